# Optimizing a Trainium2 kernel written in Bass

```python
import math
import jax, jax.numpy as jnp
from jax import lax
import numpy as np

D_MODEL = 2048
BATCH = 2
SEQ = 8192
DEPTH = 2

N_MIXERS = 2
N_SSM_LAYERS = (DEPTH + 1) // 2
N_ATTN_LAYERS = DEPTH // 2

SSM_EXPAND = 2
D_INNER = SSM_EXPAND * D_MODEL
SSM_HEAD_DIM = 64
SSM_HEADS = D_INNER // SSM_HEAD_DIM
SSM_GROUPS = 8
SSM_STATE = 128
CONV_WIDTH = 4
CHUNK = 128
CONV_DIM = D_INNER + 2 * SSM_GROUPS * SSM_STATE
IN_PROJ_DIM = D_INNER + CONV_DIM + SSM_HEADS

ATTN_GROUPS = ((128, 1), (512, 4), (2048, 16))
N_ATTN_GROUPS = len(ATTN_GROUPS)
HEADS_PER_GROUP = 16
ATTN_HEAD_DIM = 64
ATTN_WIDTH = HEADS_PER_GROUP * ATTN_HEAD_DIM
QKV_DIM = 3 * N_ATTN_GROUPS * ATTN_WIDTH

D_FF = 4 * D_MODEL
EPS = 1e-5

kernel_name = "hybrid_ssd_dilated_alibi_trunk"


def rms_norm(x, g):
    xf = x.astype(jnp.float32)
    y = xf * lax.rsqrt(jnp.mean(xf * xf, axis=-1, keepdims=True) + EPS)
    return (y * g.astype(jnp.float32)).astype(x.dtype)


def causal_depthwise_conv(u, w, b):
    out = lax.conv_general_dilated(
        u, w[:, None, :].astype(u.dtype), window_strides=(1,),
        padding=[(CONV_WIDTH - 1, 0)],
        dimension_numbers=("NWC", "WIO", "NWC"),
        feature_group_count=u.shape[-1])
    return out + b.astype(u.dtype)


def segsum_exp(a):
    T = a.shape[-1]
    cs = jnp.cumsum(a, axis=-1)
    diff = cs[..., :, None] - cs[..., None, :]
    mask = jnp.tril(jnp.ones((T, T), dtype=bool))
    return jnp.exp(jnp.where(mask, diff, -jnp.inf))


def ssd_chunked(x, a, bm, cm):
    Bsz, S, H, P = x.shape
    G, N = bm.shape[2], bm.shape[3]
    hg = H // G
    nc = S // CHUNK
    x = x.reshape(Bsz, nc, CHUNK, G, hg, P)
    a = a.reshape(Bsz, nc, CHUNK, G, hg).transpose(0, 3, 4, 1, 2)
    bm = bm.reshape(Bsz, nc, CHUNK, G, N)
    cm = cm.reshape(Bsz, nc, CHUNK, G, N)
    a_cs = jnp.cumsum(a, axis=-1)

    L = segsum_exp(a)
    cb = jnp.einsum("bclgn,bcsgn->bcgls", cm, bm)
    y_diag = jnp.einsum("bcgls,bgjcls,bcsgjp->bclgjp", cb, L, x)

    decay_states = jnp.exp(a_cs[..., -1:] - a_cs)
    states = jnp.einsum("bclgn,bgjcl,bclgjp->bcgjpn", bm, decay_states, x)
    chunk_decay = jnp.exp(a_cs[..., -1])

    def step(h, inp):
        st, dec = inp
        return h * dec[..., None, None] + st, h

    h0 = jnp.zeros((Bsz, G, hg, P, N), dtype=x.dtype)
    _, prev = lax.scan(step, h0, (states.transpose(1, 0, 2, 3, 4, 5),
                                  chunk_decay.transpose(3, 0, 1, 2)))
    prev = prev.transpose(1, 0, 2, 3, 4, 5)

    y_off = jnp.einsum("bclgn,bcgjpn,bgjcl->bclgjp", cm, prev, jnp.exp(a_cs))
    return (y_diag + y_off).reshape(Bsz, S, H, P)


def mamba2_mixer(u, w_in, conv_w, conv_b, dt_bias, a_log, d_skip, norm_w, w_out):
    Bsz, S, _ = u.shape
    zxbcdt = u @ w_in
    z = zxbcdt[..., :D_INNER]
    xbc = zxbcdt[..., D_INNER:D_INNER + CONV_DIM]
    dt = zxbcdt[..., D_INNER + CONV_DIM:]
    xbc = jax.nn.silu(causal_depthwise_conv(xbc, conv_w, conv_b))
    gn = SSM_GROUPS * SSM_STATE
    xs = xbc[..., :D_INNER].reshape(Bsz, S, SSM_HEADS, SSM_HEAD_DIM).astype(jnp.float32)
    bm = xbc[..., D_INNER:D_INNER + gn].reshape(Bsz, S, SSM_GROUPS, SSM_STATE).astype(jnp.float32)
    cm = xbc[..., D_INNER + gn:].reshape(Bsz, S, SSM_GROUPS, SSM_STATE).astype(jnp.float32)
    dt = jax.nn.softplus(dt.astype(jnp.float32) + dt_bias.astype(jnp.float32))
    A = -jnp.exp(a_log.astype(jnp.float32))
    y = ssd_chunked(xs * dt[..., None], dt * A, bm, cm)
    y = y + d_skip.astype(jnp.float32)[:, None] * xs
    y = y.reshape(Bsz, S, D_INNER) * jax.nn.silu(z.astype(jnp.float32))
    yg = y.reshape(Bsz, S, SSM_GROUPS, D_INNER // SSM_GROUPS)
    yg = yg * lax.rsqrt(jnp.mean(yg * yg, axis=-1, keepdims=True) + EPS)
    y = yg.reshape(Bsz, S, D_INNER) * norm_w.astype(jnp.float32)
    return y.astype(u.dtype) @ w_out


def alibi_slopes():
    n = N_ATTN_GROUPS * HEADS_PER_GROUP
    i = jnp.arange(1, n + 1, dtype=jnp.float32)
    return jnp.exp2(-8.0 * i / n).reshape(N_ATTN_GROUPS, HEADS_PER_GROUP)


def dilated_window_group(q, k, v, window, dilation, slopes):
    Bsz, S, H, dh = q.shape
    blk = window // dilation
    span = dilation * blk
    S_pad = -(-S // span) * span
    Lsub = S_pad // dilation
    nb = Lsub // blk

    def to_blocks(t):
        t = jnp.pad(t, ((0, 0), (0, S_pad - S), (0, 0), (0, 0)))
        t = t.reshape(Bsz, Lsub, dilation, H, dh).transpose(0, 2, 1, 3, 4)
        return t.reshape(Bsz, dilation, nb, blk, H, dh)

    def with_prev(t):
        prev = jnp.pad(t[:, :, :-1], ((0, 0), (0, 0), (1, 0), (0, 0), (0, 0), (0, 0)))
        return jnp.concatenate([prev, t], axis=3)

    qb = to_blocks(q)
    kk = with_prev(to_blocks(k))
    vv = with_prev(to_blocks(v))

    s = jnp.einsum("brnqhd,brnkhd->brnhqk", qb, kk).astype(jnp.float32) * (1.0 / math.sqrt(dh))
    qi = jnp.arange(blk)[:, None]
    kj = jnp.arange(2 * blk)[None, :]
    dist = qi + blk - kj
    valid = (dist >= 0) & (dist <= blk)
    valid = valid[None] & ((jnp.arange(nb)[:, None, None] > 0) | (kj[None] >= blk))
    bias = -slopes.astype(jnp.float32)[:, None, None] * (dist * dilation).astype(jnp.float32)[None]
    s = s + bias[None, None, None]
    s = jnp.where(valid[None, None, :, None], s, -jnp.inf)
    m = jnp.max(s, axis=-1, keepdims=True)
    p = jnp.exp(s - m)
    den = jnp.sum(p, axis=-1)
    o = jnp.einsum("brnhqk,brnkhd->brnqhd", p, vv.astype(jnp.float32))
    o = o / den.transpose(0, 1, 2, 4, 3)[..., None]
    lse = (m[..., 0] + jnp.log(den)).transpose(0, 1, 2, 4, 3)

    o = o.reshape(Bsz, dilation, Lsub, H, dh).transpose(0, 2, 1, 3, 4).reshape(Bsz, S_pad, H, dh)[:, :S]
    lse = lse.reshape(Bsz, dilation, Lsub, H).transpose(0, 2, 1, 3).reshape(Bsz, S_pad, H)[:, :S]
    return o, lse


def dilated_attention(u, w_qkv, w_o):
    Bsz, S, _ = u.shape
    qkv = (u @ w_qkv).reshape(Bsz, S, 3, N_ATTN_GROUPS, HEADS_PER_GROUP, ATTN_HEAD_DIM)
    slopes = alibi_slopes()
    outs, lses = [], []
    for g, (window, dilation) in enumerate(ATTN_GROUPS):
        o, l = dilated_window_group(qkv[:, :, 0, g], qkv[:, :, 1, g], qkv[:, :, 2, g],
                                    window, dilation, slopes[g])
        outs.append(o)
        lses.append(l)
    w = jax.nn.softmax(jnp.stack(lses, axis=0), axis=0)
    o = jnp.sum(w[..., None] * jnp.stack(outs, axis=0), axis=0)
    return o.reshape(Bsz, S, ATTN_WIDTH).astype(u.dtype) @ w_o


def sq_relu_mlp(h, w1, w2):
    a = jax.nn.relu(h @ w1)
    return (a * a) @ w2


def setup_inputs(seed: int = 0) -> dict:
    key = jax.random.key(seed)
    ks = jax.random.split(key, 20)
    f32 = jnp.float32
    nS, nA, L = N_SSM_LAYERS, N_ATTN_LAYERS, DEPTH

    def normal(k, shape, scale):
        return jax.random.normal(k, shape, f32) * scale

    x = jax.random.normal(ks[0], (BATCH, SEQ, D_MODEL), f32)
    norm_mix = 1.0 + normal(ks[1], (L, D_MODEL), 0.02)
    norm_mlp = 1.0 + normal(ks[2], (L, D_MODEL), 0.02)

    ssm_w_in = normal(ks[3], (nS, D_MODEL, IN_PROJ_DIM), D_MODEL ** -0.5)
    ssm_conv_w = normal(ks[4], (nS, CONV_WIDTH, CONV_DIM), CONV_WIDTH ** -0.5)
    ssm_conv_b = normal(ks[5], (nS, CONV_DIM), 0.02)
    dt0 = jnp.exp(jax.random.uniform(ks[6], (nS, SSM_HEADS), f32,
                                     math.log(1e-3), math.log(1e-1)))
    ssm_dt_bias = dt0 + jnp.log(-jnp.expm1(-dt0))
    ssm_a_log = jnp.log(jax.random.uniform(ks[7], (nS, SSM_HEADS), f32, 1.0, 16.0))
    ssm_d = 1.0 + normal(ks[8], (nS, SSM_HEADS), 0.02)
    ssm_norm_w = 1.0 + normal(ks[9], (nS, D_INNER), 0.02)
    ssm_w_out = normal(ks[10], (nS, D_INNER, D_MODEL), D_INNER ** -0.5)

    attn_w_qkv = normal(ks[11], (nA, D_MODEL, QKV_DIM), D_MODEL ** -0.5)
    attn_w_o = normal(ks[12], (nA, ATTN_WIDTH, D_MODEL), ATTN_WIDTH ** -0.5)

    mlp_w1 = normal(ks[13], (L, D_MODEL, D_FF), D_MODEL ** -0.5)
    mlp_w2 = normal(ks[14], (L, D_FF, D_MODEL), D_FF ** -0.5)
    final_norm = 1.0 + normal(ks[15], (D_MODEL,), 0.02)
    return {"x": x, "norm_mix": norm_mix, "norm_mlp": norm_mlp,
            "ssm_w_in": ssm_w_in, "ssm_conv_w": ssm_conv_w, "ssm_conv_b": ssm_conv_b,
            "ssm_dt_bias": ssm_dt_bias, "ssm_a_log": ssm_a_log, "ssm_d": ssm_d,
            "ssm_norm_w": ssm_norm_w, "ssm_w_out": ssm_w_out,
            "attn_w_qkv": attn_w_qkv, "attn_w_o": attn_w_o,
            "mlp_w1": mlp_w1, "mlp_w2": mlp_w2, "final_norm": final_norm}


def reference(x, norm_mix, norm_mlp, ssm_w_in, ssm_conv_w, ssm_conv_b, ssm_dt_bias,
              ssm_a_log, ssm_d, ssm_norm_w, ssm_w_out, attn_w_qkv, attn_w_o,
              mlp_w1, mlp_w2, final_norm):
    for i in range(DEPTH):
        h = rms_norm(x, norm_mix[i])
        j = i // N_MIXERS
        if i % N_MIXERS == 0:
            mix = mamba2_mixer(h, ssm_w_in[j], ssm_conv_w[j], ssm_conv_b[j], ssm_dt_bias[j],
                               ssm_a_log[j], ssm_d[j], ssm_norm_w[j], ssm_w_out[j])
        else:
            mix = dilated_attention(h, attn_w_qkv[j], attn_w_o[j])
        x = x + mix
        x = x + sq_relu_mlp(rms_norm(x, norm_mlp[i]), mlp_w1[i], mlp_w2[i])
    return rms_norm(x, final_norm)
```

```python
from contextlib import ExitStack
import numpy as np
import ml_dtypes
import concourse.bass as bass
import concourse.mybir as mybir
from concourse.bass_utils import run_bass_kernel_spmd

F32 = mybir.dt.float32
BF16 = mybir.dt.bfloat16
AF = mybir.ActivationFunctionType
ALU = mybir.AluOpType

NCORES = 8
T = 2048
NT = T // 128
TE = T + 128
D = 2048
DFF = 8192
D_INNER = 4096
NH = 64
NG = 8
CONV_DIM = 6144
IN_PROJ = 10304
EPS = 1e-5
QKV = 9216
AW = 1024


class Buf:
    __slots__ = ("w", "r")

    def __init__(self):
        self.w = None
        self.r = []


class Tok:
    __slots__ = ("sem", "key", "val", "eng")

    def __init__(self, sem, key, val, eng=None):
        self.sem, self.key, self.val, self.eng = sem, key, val, eng


class KB:
    ENGS = ("pe", "act", "dve", "pool", "sp")
    SELF_SYNC = {"pe": False, "act": True, "dve": True, "pool": True, "sp": False}

    def __init__(self, nc):
        self.nc = nc
        self.es = ExitStack()
        self.q = {e: [] for e in self.ENGS}
        self.sem = {}
        self.cnt = {}
        self.pending = {e: [] for e in self.ENGS}
        self.seen = {e: {} for e in self.ENGS}
        for e in self.ENGS:
            self.sem[e] = self.es.enter_context(nc.semaphore("sem_" + e))
            self.cnt[e] = 0
        self.nsem = 0
        self.alldsem = []
        self.barrier_deps = []
        self.stacks = []

    def sb(self, name, shape, dt):
        es = self.stacks[-1] if self.stacks else self.es
        self.nsb = getattr(self, "nsb", 0) + 1
        return es.enter_context(self.nc.sbuf_tensor(f"sb{self.nsb}_{name}", list(shape), dt))

    def ps(self, name, shape, dt):
        return self.es.enter_context(self.nc.psum_tensor(name, list(shape), dt))

    def dsem(self, name=None):
        self.nsem += 1
        nm = name or f"dsem{self.nsem}"
        s = self.es.enter_context(self.nc.semaphore(nm))
        d = [s, 0, nm]
        self.alldsem.append(d)
        return d

    def phase_begin(self):
        self.stacks.append(ExitStack())

    def phase_end(self, final_toks=()):
        self.flush(final_toks)
        if self.stacks:
            self.stacks.pop().close()
        deps = []
        for e in self.ENGS:
            if self.cnt[e] > 0:
                deps.append(Tok(self.sem[e], "e_" + e, self.cnt[e], None))
        for ds in self.alldsem:
            if ds[1] > 0:
                deps.append(Tok(ds[0], ds[2], ds[1], None))
        self.barrier_deps = deps

    def _resolve(self, eng):
        if not self.pending[eng]:
            return
        ent = self.q[eng][-1]
        if ent[2] is None:
            self.cnt[eng] += 1
            ent[2] = (self.sem[eng], 1)
        for t in self.pending[eng]:
            t.val = self.cnt[eng]
        self.pending[eng] = []

    def _deps(self, reads, writes):
        deps = list(self.barrier_deps)
        for b in reads:
            if b.w is not None:
                deps.append(b.w)
        for b in writes:
            if b.w is not None:
                deps.append(b.w)
            deps.extend(b.r)
        return deps

    def _waits(self, eng, deps):
        waits = []
        for t in deps:
            if t.eng == eng and not self.SELF_SYNC[eng]:
                continue
            if t.val is None:
                self._resolve(t.eng)
            if self.seen[eng].get(t.key, 0) >= t.val:
                continue
            self.seen[eng][t.key] = t.val
            waits.append((t.sem, t.val))
        return waits

    def op(self, eng, fn, R=(), W=()):
        waits = self._waits(eng, self._deps(R, W))
        if waits:
            self._resolve(eng)
        tok = Tok(self.sem[eng], "e_" + eng, None, eng)
        self.q[eng].append([waits, fn, None])
        self.pending[eng].append(tok)
        for b in R:
            b.r.append(tok)
        for b in W:
            b.w = tok
            b.r = []
        return tok

    def dma(self, qeng, ds, out, in_, R=(), W=(), **kw):
        waits = self._waits(qeng, self._deps(R, W))
        if waits:
            self._resolve(qeng)
        ds[1] += 16
        tok = Tok(ds[0], ds[2], ds[1], None)
        self.q[qeng].append([waits, (lambda e: e.dma_start(out=out, in_=in_, **kw)), (ds[0], 16)])
        for b in R:
            b.r.append(tok)
        for b in W:
            b.w = tok
            b.r = []
        return tok

    def dma_group(self, qeng, ds, items, R=(), W=()):
        waits = self._waits(qeng, self._deps(R, W))
        if waits:
            self._resolve(qeng)
        tok = None
        for i, (out, in_) in enumerate(items):
            ds[1] += 16
            tok = Tok(ds[0], ds[2], ds[1], None)
            self.q[qeng].append([waits if i == 0 else [],
                                 (lambda e, out=out, in_=in_: e.dma_start(out=out, in_=in_)), (ds[0], 16)])
        for b in R:
            b.r.append(tok)
        for b in W:
            b.w = tok
            b.r = []
        return tok

    def mm(self, out, lhsT, rhs, start=True, stop=True, R=(), W=()):
        return self.op("pe", lambda e: e.matmul(out, lhsT, rhs, start=start, stop=stop), R, W)

    def tr(self, out, in_, ident, R=(), W=()):
        return self.op("pe", lambda e: e.transpose(out, in_, ident), R, W)

    def act(self, out, in_, func, R=(), W=(), **kw):
        return self.op("act", lambda e: e.activation(out=out, in_=in_, func=func, **kw), R, W)

    def tt(self, eng, out, in0, in1, op, R=(), W=()):
        return self.op(eng, lambda e: e.tensor_tensor(out=out, in0=in0, in1=in1, op=op), R, W)

    def ts(self, eng, out, in0, s1, s2, op0, op1=None, R=(), W=()):
        if op1 is None:
            return self.op(eng, lambda e: e.tensor_scalar(out=out, in0=in0, scalar1=s1, scalar2=None, op0=op0), R, W)
        return self.op(eng, lambda e: e.tensor_scalar(out=out, in0=in0, scalar1=s1, scalar2=s2, op0=op0, op1=op1), R, W)

    def stt(self, out, in0, scalar, in1, op0, op1, R=(), W=()):
        return self.op("dve", lambda e: e.scalar_tensor_tensor(out=out, in0=in0, scalar=scalar, in1=in1,
                                                               op0=op0, op1=op1), R, W)

    def flush(self, final_toks=()):
        for e in self.ENGS:
            self._resolve(e)
        fw = self._waits("sp", list(final_toks))

        def replay(eng_name):
            def run(e):
                for waits, fn, inc in self.q[eng_name]:
                    for s, v in waits:
                        e.wait_ge(s, v)
                    ins = fn(e)
                    if inc is not None:
                        ins.then_inc(inc[0], inc[1])
                if eng_name == "sp":
                    for s, v in fw:
                        e.wait_ge(s, v)
            return run

        with self.nc.Block() as block:
            block.tensor(replay("pe"))
            block.scalar(replay("act"))
            block.vector(replay("dve"))
            block.gpsimd(replay("pool"))
            block.sync(replay("sp"))
        self.q = {e: [] for e in self.ENGS}

    def finish(self, final_toks=()):
        self.phase_end(final_toks)
        self.es.close()


class Prog:
    def __init__(self, io):
        self.nc = bass.Bass("TRN2", target_bir_lowering=False)
        self.K = KB(self.nc)
        self.io = io
        self.d = {}
        self.dbufs = {}
        self.finals = []
        self.sem_pool = []
        self.sem_i = 0
        K = self.K
        self.cst = self.dram("consts", [128, 5 * 128], BF16)
        self.cstf = self.dram("constsf", [128, 128 + 1024 + 64], F32)
        self.ident = K.sb("ident", [128, 128], BF16)
        self.Ut = K.sb("Ut", [128, 128], BF16)
        self.Ls = K.sb("Ls", [128, 128], BF16)
        self.ones = K.sb("ones", [128, 128], BF16)
        self.maskf = K.sb("maskf", [128, 128], F32)
        self.dist = K.sb("dist", [128, 1024], F32)
        self.eps_t = K.sb("eps_t", [128, 1], F32)
        self.b_cst = Buf()
        ds = K.dsem("ds_cst")
        for i, t in enumerate((self.ident, self.Ut, self.Ls, self.ones)):
            K.dma("sp", ds, t[:], self.cst[:, i * 128:(i + 1) * 128], W=[self.b_cst])
        K.dma("sp", ds, self.maskf[:], self.cstf[:, 0:128], W=[self.b_cst])
        K.dma("sp", ds, self.dist[:], self.cstf[:, 128:1152], W=[self.b_cst])
        K.op("dve", lambda e: e.memset(self.eps_t[:], EPS), W=[self.b_cst])
        self.bank = [K.ps(f"bank{i}", [128, 512], F32) for i in range(8)]
        self.b_bank = [Buf() for _ in range(8)]
        self.rr = 0

    def dram(self, name, shape, dt):
        role = self.io.get(name, "int")
        kind = {"in": "ExternalInput", "out": "ExternalOutput", "int": "Internal"}[role]
        ap = self.nc.dram_tensor(name, list(shape), dt, kind=kind).ap()
        self.d[name] = ap
        return ap

    def db(self, name, idx=0):
        return self.dbufs.setdefault((name, idx), Buf())

    def begin(self):
        self.K.phase_begin()
        self.sem_i = 0

    def end(self):
        self.K.phase_end()

    def gs(self):
        if self.sem_i == len(self.sem_pool):
            self.sem_pool.append(self.K.dsem(f"dp{self.sem_i}"))
        s = self.sem_pool[self.sem_i]
        self.sem_i += 1
        return s

    def store(self, ds, out, in_, R=(), W=(), q="sp"):
        t = self.K.dma(q, ds, out, in_, R=R, W=W)
        self.finals.append(t)
        return t

    def bcast_load(self, name, row_ap, n, dt=F32):
        t = self.K.sb(name, [128, n], dt)
        b = Buf()
        self.K.dma("sp", self.gs(), t[:], row_ap.to_broadcast([128, n]), W=[b])
        return t, b

    def norm(self, x_d, ntiles, g_row, hT, b_hT, tok0=0):
        K = self.K
        self.begin()
        gB, b_gB = self.bcast_load("gB", g_row, D)
        xt = [K.sb(f"xt{i}", [128, D], F32) for i in range(2)]
        xn = [K.sb(f"xn{i}", [128, D], BF16) for i in range(2)]
        ssq = [K.sb(f"ssq{i}", [128, 1], F32) for i in range(2)]
        rs = [K.sb(f"rs{i}", [128, 1], F32) for i in range(2)]
        b_xt = [Buf() for _ in range(2)]
        b_xn = [Buf() for _ in range(2)]
        b_sm = [Buf() for _ in range(2)]
        dsx = [self.gs() for _ in range(2)]
        for t in range(ntiles):
            s = t % 2
            K.dma("sp", dsx[s], xt[s][:], x_d[t * 128:(t + 1) * 128, :], W=[b_xt[s]])
            K.act(xn[s][:], xt[s][:], AF.Square, R=[b_xt[s]], W=[b_xn[s], b_sm[s]], accum_out=ssq[s][:])
            K.act(rs[s][:], ssq[s][:], AF.Sqrt, R=[b_sm[s], self.b_cst], W=[b_sm[s]], scale=1.0 / D, bias=self.eps_t[:])
            K.op("dve", lambda e, s=s: e.reciprocal(out=rs[s][:], in_=rs[s][:]), R=[b_sm[s]], W=[b_sm[s]])
            K.stt(xn[s][:], xt[s][:], rs[s][:], gB[:], ALU.mult, ALU.mult, R=[b_xt[s], b_sm[s], b_gB], W=[b_xn[s]])
            for half in range(2):
                bi = 4 + 2 * (t % 2) + half
                pv = self.bank[bi][:].bitcast(BF16)
                for j in range(8):
                    kc = half * 8 + j
                    K.tr(pv[:, j * 128:(j + 1) * 128], xn[s][:, kc * 128:(kc + 1) * 128], self.ident[:],
                         R=[b_xn[s], self.b_cst], W=[self.b_bank[bi]])
                K.act(hT[:, half * 8:(half + 1) * 8, tok0 + t * 128:tok0 + (t + 1) * 128],
                      pv.rearrange("p (j q) -> p j q", j=8), AF.Copy, R=[self.b_bank[bi]], W=[b_hT])
        self.end()

    def dense_A(self, hT, b_hT, w_d, col0, ncols, slices, epi, post=None, nslot=3):
        K = self.K
        slab = [K.sb(f"slabA{i}", [128, 16 * 512], BF16) for i in range(nslot)]
        b_slab = [Buf() for _ in range(nslot)]
        ds_slab = [self.gs() for _ in range(nslot)]
        wv = w_d.rearrange("(kc p) n -> p kc n", p=128)
        for sl in range(ncols // 512):
            s = sl % nslot
            sv = slab[s][:].rearrange("p (kc n) -> p kc n", kc=16)
            K.dma("pool", ds_slab[s], sv, wv[:, :, col0 + sl * 512:col0 + (sl + 1) * 512], W=[b_slab[s]])
            for cb in range(4):
                cbk = sl * 4 + cb
                for (t0, n) in slices:
                    bi = self.rr % 4
                    self.rr += 1
                    for kc in range(16):
                        K.mm(self.bank[bi][:, 0:n], sv[:, kc, cb * 128:(cb + 1) * 128], hT[:, kc, t0:t0 + n],
                             kc == 0, kc == 15, R=[b_slab[s], b_hT], W=[self.b_bank[bi]])
                    epi(cbk, bi, t0, n)
                if post is not None:
                    post(cbk)

    def dense_B(self, aT_d, aT_name, KC, w_d, res_d, out_d, out_name):
        K = self.K
        self.begin()
        HK = min(KC, 32)
        nh = KC // HK
        aTb = K.sb("aTb", [128, KC, 512], BF16)
        b_aTb = Buf()
        ds_aTb = self.gs()
        hslab = [K.sb(f"hslab{i}", [128, HK * 512], BF16) for i in range(2)]
        b_hslab = [Buf() for _ in range(2)]
        ds_hslab = [self.gs() for _ in range(2)]
        xr = [K.sb(f"xr{i}", [128, 512], F32) for i in range(2)]
        b_xr = [Buf() for _ in range(2)]
        ds_xr = [self.gs() for _ in range(2)]
        ost = [K.sb(f"ost{i}", [128, 512], F32) for i in range(2)]
        b_ost = [Buf() for _ in range(2)]
        ds_ost = [self.gs() for _ in range(2)]
        wv = w_d.rearrange("(kc p) n -> p kc n", p=128)
        aTv = aT_d.rearrange("(kc p) t -> p kc t", p=128)
        hs = 0
        ep = 0
        for tb in range(T // 512):
            for q4 in range(KC // 16):
                K.dma("sp", ds_aTb, aTb[:, q4 * 16:(q4 + 1) * 16, :],
                      aTv[:, q4 * 16:(q4 + 1) * 16, tb * 512:(tb + 1) * 512],
                      R=[self.db(aT_name, i) for i in range(q4 * 16, (q4 + 1) * 16)], W=[b_aTb])
            for cb in range(4):
                bset = 4 * (cb % 2)
                for half in range(nh):
                    s = hs % 2
                    hs += 1
                    sv = hslab[s][:].rearrange("p (kc n) -> p kc n", kc=HK)
                    K.dma("pool", ds_hslab[s], sv, wv[:, half * HK:(half + 1) * HK, cb * 512:(cb + 1) * 512],
                          W=[b_hslab[s]])
                    for tt in range(4):
                        bi = bset + tt
                        for kc in range(HK):
                            K.mm(self.bank[bi][:], aTb[:, half * HK + kc, tt * 128:(tt + 1) * 128], sv[:, kc, :],
                                 half == 0 and kc == 0, half == nh - 1 and kc == HK - 1,
                                 R=[b_hslab[s], b_aTb], W=[self.b_bank[bi]])
                for tt in range(4):
                    bi = bset + tt
                    e2 = ep % 2
                    ep += 1
                    r0 = tb * 512 + tt * 128
                    K.dma("sp", ds_xr[e2], xr[e2][:], res_d[r0:r0 + 128, cb * 512:(cb + 1) * 512],
                          R=[self.db("res_" + out_name, 0)], W=[b_xr[e2]])
                    K.tt("dve", ost[e2][:], self.bank[bi][:], xr[e2][:], ALU.add,
                         R=[self.b_bank[bi], b_xr[e2]], W=[b_ost[e2]])
                    self.store(ds_ost[e2], out_d[r0:r0 + 128, cb * 512:(cb + 1) * 512], ost[e2][:],
                               R=[b_ost[e2]], W=[self.db(out_name, tb * 4 + tt)])
        self.end()

    def mlp(self, x_in, x_in_name, g_row, w1_d, w2_d, x_out, x_out_name):
        K = self.K
        K.phase_begin()
        hT = K.sb("hT", [128, 16, T], BF16)
        b_hT = Buf()
        self.norm(x_in, NT, g_row, hT, b_hT)
        aT_d = self.d.get("aT_scr")
        if aT_d is None:
            aT_d = self.dram("aT_scr", [DFF, T], BF16)
        self.begin()
        tmp = [K.sb(f"tmp{i}", [128, 512], F32) for i in range(2)]
        b_tmp = [Buf() for _ in range(2)]
        stg = [K.sb(f"stg{i}", [128, T], BF16) for i in range(2)]
        b_stg = [Buf() for _ in range(2)]
        ds_stg = [self.gs() for _ in range(2)]
        st = {"i": 0}

        def epi(cbk, bi, t0, n):
            tm = st["i"] % 2
            st["i"] += 1
            sg = cbk % 2
            K.ts("dve", tmp[tm][:], self.bank[bi][:], 0.0, None, ALU.max, R=[self.b_bank[bi]], W=[b_tmp[tm]])
            K.act(stg[sg][:, t0:t0 + n], tmp[tm][:], AF.Square, R=[b_tmp[tm]], W=[b_stg[sg]])

        def post(cbk):
            sg = cbk % 2
            self.store(ds_stg[sg], aT_d[cbk * 128:(cbk + 1) * 128, :], stg[sg][:], R=[b_stg[sg]],
                       W=[self.db("aT_scr", cbk)])

        self.dense_A(hT, b_hT, w1_d, 0, DFF, [(i * 512, 512) for i in range(4)], epi, post)
        self.end()
        K.phase_end()
        self.dense_B(aT_d, "aT_scr", 64, w2_d, x_in, x_out, x_out_name)

    def ssm_scratch(self):
        self.Zs = self.dram("Zs", [T, D_INNER], BF16)
        self.Xs = self.dram("Xs", [T, D_INNER], BF16)
        self.Bt = self.dram("Bt", [T, 1024], BF16)
        self.BT = self.dram("BT", [1024, T], BF16)
        self.CT = self.dram("CT", [1024, T], BF16)
        self.dtd = self.dram("dt_s", [T, NH], F32)
        self.ad = self.dram("a_s", [T, NH], F32)

    def ssm_stage_A(self, x_ext, g_row, w_in, conv_wT, conv_bT, dtb_row, alog_row):
        K = self.K
        K.phase_begin()
        hT = K.sb("hT", [128, 16, TE], BF16)
        b_hT = Buf()
        self.norm(x_ext, NT + 1, g_row, hT, b_hT)

        self.begin()
        cw = K.sb("cw", [128, 48 * 4], F32)
        cbias = K.sb("cbias", [128, 48], F32)
        b_cw = Buf()
        dsc = self.gs()
        K.dma("sp", dsc, cw[:], conv_wT, W=[b_cw])
        K.dma("sp", dsc, cbias[:], conv_bT, W=[b_cw])
        u = [K.sb(f"u{i}", [128, TE], F32) for i in range(2)]
        acc = [K.sb(f"acc{i}", [128, T], F32) for i in range(2)]
        o = [K.sb(f"o{i}", [128, T], BF16) for i in range(2)]
        b_u = [Buf() for _ in range(2)]
        b_acc = [Buf() for _ in range(2)]
        b_o = [Buf() for _ in range(2)]
        ds_o = [self.gs() for _ in range(2)]
        xst = [K.sb(f"xst{i}", [128, 16, 256], BF16) for i in range(2)]
        b_xst = [Buf() for _ in range(2)]
        ds_xst = [self.gs() for _ in range(2)]
        bst = K.sb("bst", [128, 16, 128], BF16)
        b_bst = Buf()
        ds_bst = self.gs()
        Xs_v = self.Xs.rearrange("(t p) c -> p t c", p=128)
        Bt_v = self.Bt.rearrange("(t p) c -> p t c", p=128)
        trr = {"i": 0}

        def epi(cbk, bi, t0, n):
            ui = cbk % 2
            K.act(u[ui][:, t0:t0 + n], self.bank[bi][:, 0:n], AF.Copy, R=[self.b_bank[bi]], W=[b_u[ui]])

        def to_tokmajor(src, b_src, dst, b_dst, c0):
            for half in range(2):
                bi = 4 + trr["i"] % 4
                trr["i"] += 1
                pv = self.bank[bi][:].bitcast(BF16)
                for j in range(8):
                    tt = half * 8 + j
                    K.tr(pv[:, j * 128:(j + 1) * 128], src[:, tt * 128:(tt + 1) * 128], self.ident[:],
                         R=[b_src, self.b_cst], W=[self.b_bank[bi]])
                K.op("dve", lambda e, pv=pv, half=half: e.tensor_copy(
                    out=dst[:, half * 8:(half + 1) * 8, c0:c0 + 128], in_=pv.rearrange("p (j q) -> p j q", j=8)),
                    R=[self.b_bank[bi]], W=[b_dst])

        def post(cbk):
            ui = cbk % 2
            K.act(acc[ui][:], u[ui][:, 128:TE], AF.Identity, R=[b_u[ui], b_cw], W=[b_acc[ui]],
                  scale=cw[:, cbk * 4 + 3:cbk * 4 + 4], bias=cbias[:, cbk:cbk + 1])
            for k in range(3):
                K.stt(acc[ui][:], u[ui][:, 125 + k:125 + k + T], cw[:, cbk * 4 + k:cbk * 4 + k + 1], acc[ui][:],
                      ALU.mult, ALU.add, R=[b_u[ui], b_acc[ui], b_cw], W=[b_acc[ui]])
            K.act(o[ui][:], acc[ui][:], AF.Silu, R=[b_acc[ui]], W=[b_o[ui]])
            if cbk < 32:
                pr = (cbk // 2) % 2
                to_tokmajor(o[ui], b_o[ui], xst[pr], b_xst[pr], (cbk % 2) * 128)
                if cbk % 2 == 1:
                    c0 = (cbk - 1) * 128
                    self.store(ds_xst[pr], Xs_v[:, :, c0:c0 + 256], xst[pr][:], R=[b_xst[pr]],
                               W=[self.db("Xs", cbk // 2)])
            elif cbk < 40:
                g = cbk - 32
                self.store(ds_o[ui], self.BT[g * 128:(g + 1) * 128, :], o[ui][:], R=[b_o[ui]], W=[self.db("BT", g)])
                to_tokmajor(o[ui], b_o[ui], bst, b_bst, 0)
                self.store(ds_bst, Bt_v[:, :, g * 128:(g + 1) * 128], bst[:], R=[b_bst], W=[self.db("Bt", g)])
            else:
                g = cbk - 40
                self.store(ds_o[ui], self.CT[g * 128:(g + 1) * 128, :], o[ui][:], R=[b_o[ui]], W=[self.db("CT", g)])

        slices = [(0, 128)] + [(128 + 512 * i, 512) for i in range(4)]
        self.dense_A(hT, b_hT, w_in, D_INNER, CONV_DIM, slices, epi, post, nslot=2)
        self.end()

        self.begin()
        wv = w_in.rearrange("(kc p) n -> p kc n", p=128)
        dtb, b_dtb = self.bcast_load("dtb", dtb_row, NH)
        Abc, b_Abc = self.bcast_load("Abc", alog_row, NH)
        K.act(Abc[:], Abc[:], AF.Exp, R=[b_Abc], W=[b_Abc])
        K.ts("dve", Abc[:], Abc[:], -1.0, None, ALU.mult, R=[b_Abc], W=[b_Abc])
        sdt = K.sb("sdt", [128, 16 * NH], BF16)
        b_sdt = Buf()
        sdv = sdt[:].rearrange("p (kc n) -> p kc n", kc=16)
        K.dma("pool", self.gs(), sdv, wv[:, :, IN_PROJ - NH:IN_PROJ], W=[b_sdt])
        dt_all = K.sb("dt_all", [128, NT, NH], F32)
        a_all = K.sb("a_all", [128, NT, NH], F32)
        b_dta = Buf()
        tv = [K.sb(f"tv{i}", [128, NH], F32) for i in range(2)]
        b_tv = [Buf() for _ in range(2)]
        for tt in range(NT):
            bi = self.rr % 4
            self.rr += 1
            i2 = tt % 2
            for kc in range(16):
                K.mm(self.bank[bi][:, 0:NH], hT[:, kc, 128 + tt * 128:128 + (tt + 1) * 128], sdv[:, kc, :],
                     kc == 0, kc == 15, R=[b_hT, b_sdt], W=[self.b_bank[bi]])
            K.tt("dve", tv[i2][:], self.bank[bi][:, 0:NH], dtb[:], ALU.add, R=[self.b_bank[bi], b_dtb], W=[b_tv[i2]])
            K.act(tv[i2][:], tv[i2][:], AF.Exp, R=[b_tv[i2]], W=[b_tv[i2]])
            K.act(dt_all[:, tt, :], tv[i2][:], AF.Ln, R=[b_tv[i2]], W=[b_dta], bias=1.0)
            K.tt("dve", a_all[:, tt, :], dt_all[:, tt, :], Abc[:], ALU.mult, R=[b_dta, b_Abc], W=[b_dta])
        dsd = self.gs()
        self.store(dsd, self.dtd.rearrange("(t p) h -> p t h", p=128), dt_all[:], R=[b_dta], W=[self.db("dt_s")])
        self.store(dsd, self.ad.rearrange("(t p) h -> p t h", p=128), a_all[:], R=[b_dta], W=[self.db("a_s")])
        nslot = 2
        slab = [K.sb(f"slabZ{i}", [128, 16 * 512], BF16) for i in range(nslot)]
        b_slab = [Buf() for _ in range(nslot)]
        ds_slab = [self.gs() for _ in range(nslot)]
        zst = [K.sb(f"zst{i}", [128, 512], BF16) for i in range(2)]
        b_zst = [Buf() for _ in range(2)]
        ds_zst = [self.gs() for _ in range(2)]
        zi = 0
        for sl in range(D_INNER // 512):
            s = sl % nslot
            sv = slab[s][:].rearrange("p (kc n) -> p kc n", kc=16)
            K.dma("pool", ds_slab[s], sv, wv[:, :, sl * 512:(sl + 1) * 512], W=[b_slab[s]])
            for tt in range(NT):
                bi = self.rr % 4
                self.rr += 1
                for kc in range(16):
                    K.mm(self.bank[bi][:], hT[:, kc, 128 + tt * 128:128 + (tt + 1) * 128], sv[:, kc, :],
                         kc == 0, kc == 15, R=[b_hT, b_slab[s]], W=[self.b_bank[bi]])
                z2 = zi % 2
                zi += 1
                K.act(zst[z2][:], self.bank[bi][:], AF.Silu, R=[self.b_bank[bi]], W=[b_zst[z2]])
                self.store(ds_zst[z2], self.Zs[tt * 128:(tt + 1) * 128, sl * 512:(sl + 1) * 512], zst[z2][:],
                           R=[b_zst[z2]], W=[self.db("Zs", sl * NT + tt)])
        self.end()
        K.phase_end()

    def ssm_scan(self, full, H, b_H, cdtot=None, b_cdtot=None, d_row=None, normw_row=None):
        K = self.K
        self.begin()
        NB = 2
        Xc = [K.sb(f"Xc{i}", [128, D_INNER], BF16) for i in range(NB)]
        Btc = [K.sb(f"Btc{i}", [128, 1024], BF16) for i in range(NB)]
        dtc = [K.sb(f"dtc{i}", [128, NH], F32) for i in range(NB)]
        ac = [K.sb(f"ac{i}", [128, NH], F32) for i in range(NB)]
        b_in = [Buf() for _ in range(NB)]
        ds_in = [self.gs() for _ in range(NB)]
        if full:
            BTc = [K.sb(f"BTc{i}", [128, NG, 128], BF16) for i in range(NB)]
            CTc = [K.sb(f"CTc{i}", [128, NG, 128], BF16) for i in range(NB)]
            Zc = [K.sb(f"Zc{i}", [128, D_INNER], BF16) for i in range(NB)]
            Dbc, b_Dbc = self.bcast_load("Dbc", d_row, NH)
            nw, b_nw = self.bcast_load("nw", normw_row, D_INNER)
            Hbf = K.sb("Hbf", [128, D_INNER], BF16)
            b_Hbf = Buf()
            K.op("pool", lambda e: e.tensor_copy(out=Hbf[:], in_=H[:]), R=[b_H], W=[b_Hbf])
            Xd = K.sb("Xd", [128, D_INNER], BF16)
            b_Xd = Buf()
            CBm = [K.sb(f"CBm{i}", [128, 128], BF16) for i in range(2)]
            b_CBm = [Buf() for _ in range(2)]
            rhs4 = [K.sb(f"rhs4{i}", [128, 4, 128], BF16) for i in range(2)]
            b_rhs4 = [Buf() for _ in range(2)]
            E4 = [K.sb(f"E4{i}", [128, 4, 128], BF16) for i in range(2)]
            b_E4 = [Buf() for _ in range(2)]
            M4 = [K.sb(f"M4{i}", [128, 4, 128], BF16) for i in range(2)]
            b_M4 = [Buf() for _ in range(2)]
            t1 = K.sb("t1", [128, 512], F32)
            t2 = K.sb("t2", [128, 512], F32)
            yv = K.sb("yv", [128, 512], F32)
            junk = K.sb("junk", [128, 512], BF16)
            ynb = K.sb("ynb", [128, 512], BF16)
            gss = K.sb("gss", [128, 1], F32)
            grs = K.sb("grs", [128, 1], F32)
            b_t1, b_t2, b_yv, b_ynb, b_gs = Buf(), Buf(), Buf(), Buf(), Buf()
            ynT = [K.sb(f"ynT{i}", [128, 4, 128], BF16) for i in range(2)]
            b_ynT = [Buf() for _ in range(2)]
            ds_ynT = [self.gs() for _ in range(2)]
            YnT_v = self.YnT.rearrange("(b p) t -> p b t", p=128)
        Xsc = K.sb("Xsc", [128, D_INNER], BF16)
        b_Xsc = Buf()
        abf = K.sb("abf", [128, NH], BF16)
        b_abf = Buf()
        ex = K.sb("ex", [128, 3 * NH], F32)
        b_ex = Buf()
        sc = K.sb("sc", [128, NH], F32)
        b_sc = Buf()
        BT_v = self.BT.rearrange("(g n) t -> n g t", n=128)
        CT_v = self.CT.rearrange("(g n) t -> n g t", n=128)

        def load(c):
            s = c % NB
            r = slice(c * 128, (c + 1) * 128)
            items = [(Xc[s][:], self.Xs[r, :]), (Btc[s][:], self.Bt[r, :]), (dtc[s][:], self.dtd[r, :]),
                     (ac[s][:], self.ad[r, :])]
            if full:
                items += [(BTc[s][:], BT_v[:, :, r]), (CTc[s][:], CT_v[:, :, r]), (Zc[s][:], self.Zs[r, :])]
            K.dma_group("sp", ds_in[s], items, W=[b_in[s]])

        load(0)
        for c in range(NT):
            s = c % NB
            if c + 1 < NT:
                load(c + 1)
            bi_s = [b_in[s]]
            K.op("dve", lambda e, s=s: e.tensor_copy(out=abf[:], in_=ac[s][:]), R=bi_s, W=[b_abf])
            mb = self.bank[0]
            K.mm(mb[:, 0:NH], self.ones[:], abf[:], R=[b_abf, self.b_cst], W=[self.b_bank[0]])
            K.mm(mb[:, NH:2 * NH], self.Ls[:], abf[:], R=[b_abf, self.b_cst], W=[self.b_bank[0]])
            K.mm(mb[:, 2 * NH:3 * NH], self.Ut[:], abf[:], R=[b_abf, self.b_cst], W=[self.b_bank[0]])
            K.act(ex[:], mb[:, 0:3 * NH], AF.Exp, R=[self.b_bank[0]], W=[b_ex])
            K.tt("dve", sc[:], dtc[s][:], ex[:, NH:2 * NH], ALU.mult, R=bi_s + [b_ex], W=[b_sc])
            Xv = Xc[s][:].rearrange("p (h d) -> p h d", h=NH)
            K.tt("pool", Xsc[:].rearrange("p (h d) -> p h d", h=NH), Xv,
                 sc[:].unsqueeze(2).broadcast_to([128, NH, 64]), ALU.mult, R=bi_s + [b_sc], W=[b_Xsc])
            if cdtot is not None:
                K.tt("dve", cdtot[:], cdtot[:], ex[:, 0:NH], ALU.mult, R=[b_ex, b_cdtot], W=[b_cdtot])
            if full:
                K.tt("pool", Xd[:].rearrange("p (h d) -> p h d", h=NH), Xv,
                     dtc[s][:].unsqueeze(2).broadcast_to([128, NH, 64]), ALU.mult, R=bi_s, W=[b_Xd])
            for g in range(NG):
                gsl = slice(g * 512, (g + 1) * 512)
                if full:
                    cbi = g % 2
                    K.mm(self.bank[1][:, 0:128], BTc[s][:, g, :], CTc[s][:, g, :], R=bi_s, W=[self.b_bank[1]])
                    K.tt("dve", CBm[cbi][:], self.bank[1][:, 0:128], self.maskf[:], ALU.mult,
                         R=[self.b_bank[1], self.b_cst], W=[b_CBm[cbi]])
                    yb = self.bank[4]
                    for half in range(2):
                        i2 = (g * 2 + half) % 2
                        h0 = g * 8 + half * 4
                        K.tt("pool", rhs4[i2][:], self.Ut[:].unsqueeze(1).broadcast_to([128, 4, 128]),
                             ac[s][:, h0:h0 + 4].unsqueeze(2).broadcast_to([128, 4, 128]), ALU.mult,
                             R=bi_s + [self.b_cst], W=[b_rhs4[i2]])
                        db_ = 2 + i2
                        K.mm(self.bank[db_][:], self.Ls[:], rhs4[i2][:].rearrange("p a b -> p (a b)"),
                             R=[b_rhs4[i2], self.b_cst], W=[self.b_bank[db_]])
                        K.act(E4[i2][:].rearrange("p a b -> p (a b)"), self.bank[db_][:], AF.Exp,
                              R=[self.b_bank[db_]], W=[b_E4[i2]])
                        K.tt("dve", M4[i2][:], E4[i2][:], CBm[cbi][:].unsqueeze(1).broadcast_to([128, 4, 128]),
                             ALU.mult, R=[b_E4[i2], b_CBm[cbi]], W=[b_M4[i2]])
                        for hh in range(4):
                            h = h0 + hh
                            K.mm(yb[:, (half * 4 + hh) * 64:(half * 4 + hh + 1) * 64], M4[i2][:, hh, :],
                                 Xd[:, h * 64:(h + 1) * 64], R=[b_M4[i2], b_Xd], W=[self.b_bank[4]])
                    K.mm(self.bank[5][:], CTc[s][:, g, :], Hbf[:, gsl], R=bi_s + [b_Hbf], W=[self.b_bank[5]])
                    hb = slice(g * 8, (g + 1) * 8)
                    K.tt("dve", t1[:].rearrange("p (h d) -> p h d", h=8),
                         self.bank[5][:].rearrange("p (h d) -> p h d", h=8),
                         ex[:, 2 * NH + g * 8:2 * NH + (g + 1) * 8].unsqueeze(2).broadcast_to([128, 8, 64]),
                         ALU.mult, R=[self.b_bank[5], b_ex], W=[b_t1])
                    K.tt("pool", t2[:].rearrange("p (h d) -> p h d", h=8),
                         Xc[s][:, gsl].rearrange("p (h d) -> p h d", h=8),
                         Dbc[:, hb].unsqueeze(2).broadcast_to([128, 8, 64]), ALU.mult,
                         R=bi_s + [b_Dbc], W=[b_t2])
                    K.tt("pool", t2[:], t2[:], t1[:], ALU.add, R=[b_t1, b_t2], W=[b_t2])
                    K.tt("dve", yv[:], self.bank[4][:], t2[:], ALU.add, R=[self.b_bank[4], b_t2], W=[b_yv])
                    K.tt("dve", yv[:], yv[:], Zc[s][:, gsl], ALU.mult, R=[b_yv] + bi_s, W=[b_yv])
                    K.act(junk[:], yv[:], AF.Square, R=[b_yv], W=[b_gs], accum_out=gss[:])
                    K.act(grs[:], gss[:], AF.Sqrt, R=[b_gs, self.b_cst], W=[b_gs], scale=1.0 / 512, bias=self.eps_t[:])
                    K.op("dve", lambda e: e.reciprocal(out=grs[:], in_=grs[:]), R=[b_gs], W=[b_gs])
                    K.stt(ynb[:], yv[:], grs[:], nw[:, gsl], ALU.mult, ALU.mult, R=[b_yv, b_gs, b_nw], W=[b_ynb])
                    pv = self.bank[7][:].bitcast(BF16)
                    for j in range(4):
                        K.tr(pv[:, j * 128:(j + 1) * 128], ynb[:, j * 128:(j + 1) * 128], self.ident[:],
                             R=[b_ynb, self.b_cst], W=[self.b_bank[7]])
                    y2 = g % 2
                    K.act(ynT[y2][:], pv[:, 0:512].rearrange("p (j q) -> p j q", j=4), AF.Copy,
                          R=[self.b_bank[7]], W=[b_ynT[y2]])
                    self.store(ds_ynT[y2], YnT_v[:, g * 4:(g + 1) * 4, c * 128:(c + 1) * 128], ynT[y2][:],
                               R=[b_ynT[y2]], W=[self.db("YnT", g)])
                K.mm(self.bank[6][:], Btc[s][:, g * 128:(g + 1) * 128], Xsc[:, gsl], R=bi_s + [b_Xsc],
                     W=[self.b_bank[6]])
                Hg = H[:, gsl]
                K.tt("pool", Hg.rearrange("p (h d) -> p h d", h=8), Hg.rearrange("p (h d) -> p h d", h=8),
                     ex[:, g * 8:(g + 1) * 8].unsqueeze(2).broadcast_to([128, 8, 64]), ALU.mult,
                     R=[b_ex] + ([b_Hbf] if full else []), W=[b_H])
                K.tt("dve", Hg, Hg, self.bank[6][:], ALU.add, R=[self.b_bank[6]], W=[b_H])
                if full:
                    K.act(Hbf[:, gsl], Hg, AF.Copy, R=[b_H], W=[b_Hbf])
        self.end()

    def ssm_combine(self, Sprev, Pprev, H, b_H):
        K = self.K
        self.begin()
        St = [K.sb(f"St{i}", [128, D_INNER], F32) for i in range(2)]
        Pt = [K.sb(f"Pt{i}", [128, NH], F32) for i in range(2)]
        b_S = [Buf() for _ in range(2)]
        ds = [self.gs() for _ in range(2)]
        K.op("dve", lambda e: e.memset(H[:], 0.0), W=[b_H])
        Hv = H[:].rearrange("p (h d) -> p h d", h=NH)
        for i in range(3):
            s = i % 2
            K.dma_group("sp", ds[s], [(St[s][:], Sprev[i]), (Pt[s][:], Pprev[i])], W=[b_S[s]])
            K.tt("dve", Hv, Hv, Pt[s][:].unsqueeze(2).broadcast_to([128, NH, 64]), ALU.mult, R=[b_S[s]], W=[b_H])
            K.tt("dve", H[:], H[:], St[s][:], ALU.add, R=[b_S[s]], W=[b_H])
        self.end()


def _consts():
    j = np.arange(128)
    ident = np.eye(128, dtype=np.float32)
    Ut = (j[:, None] <= j[None, :]).astype(np.float32)
    Ls = (j[:, None] > j[None, :]).astype(np.float32)
    ones = np.ones((128, 128), np.float32)
    zero = np.zeros((128, 128), np.float32)
    cst = np.concatenate([ident, Ut, Ls, ones, zero], axis=1).astype(ml_dtypes.bfloat16)
    k = j[:, None]
    q = j[None, :]
    cur = np.where(q - k >= 0, q - k, 1e6).astype(np.float32)
    prev = np.where(k >= q, q + 128 - k, 1e6).astype(np.float32)
    shf = (j[:, None] == (np.arange(64)[None, :] + 64)).astype(np.float32)
    cstf = np.concatenate([Ut, np.tile(prev, (1, 4)), np.tile(cur, (1, 4)), shf], axis=1).astype(np.float32)
    return cst, cstf


def build_L1():
    io = {n: "in" for n in ("consts", "constsf", "x_ext", "g0", "w_in", "conv_wT", "conv_bT", "dtb", "alog")}
    io.update({n: "out" for n in ("Zs", "Xs", "Bt", "BT", "CT", "dt_s", "a_s", "Sloc", "Ptot")})
    P = Prog(io)
    K = P.K
    x_ext = P.dram("x_ext", [TE, D], F32)
    g0 = P.dram("g0", [1, D], F32)
    w_in = P.dram("w_in", [D, IN_PROJ], F32)
    conv_wT = P.dram("conv_wT", [128, 48 * 4], F32)
    conv_bT = P.dram("conv_bT", [128, 48], F32)
    dtb = P.dram("dtb", [1, NH], F32)
    alog = P.dram("alog", [1, NH], F32)
    Sloc = P.dram("Sloc", [128, D_INNER], F32)
    Ptot = P.dram("Ptot", [128, NH], F32)
    P.ssm_scratch()
    P.ssm_stage_A(x_ext, g0, w_in, conv_wT, conv_bT, dtb, alog)
    K.phase_begin()
    H = K.sb("H", [128, D_INNER], F32)
    cdtot = K.sb("cdtot", [128, NH], F32)
    b_H, b_cd = Buf(), Buf()
    K.op("dve", lambda e: e.memset(H[:], 0.0), W=[b_H])
    K.op("dve", lambda e: e.memset(cdtot[:], 1.0), W=[b_cd])
    P.ssm_scan(False, H, b_H, cdtot, b_cd)
    ds = P.gs()
    P.store(ds, Sloc, H[:], R=[b_H])
    P.store(ds, Ptot, cdtot[:], R=[b_cd])
    K.phase_end()
    K.finish(P.finals)
    return P.nc


def build_L2(with_qkv=True):
    ins = ("consts", "constsf", "Zs", "Xs", "Bt", "BT", "CT", "dt_s", "a_s", "Sprev", "Pprev", "x_own", "dskip",
           "normw", "w_out", "g_mlp0", "w1_0", "w2_0", "g_mix1", "w_qkv")
    io = {n: "in" for n in ins}
    io.update({n: "out" for n in ("x2", "QT", "KT", "Vaug")})
    P = Prog(io)
    K = P.K
    P.ssm_scratch()
    Sprev = P.dram("Sprev", [3, 128, D_INNER], F32)
    Pprev = P.dram("Pprev", [3, 128, NH], F32)
    x_own = P.dram("x_own", [T, D], F32)
    dskip = P.dram("dskip", [1, NH], F32)
    normw = P.dram("normw", [1, D_INNER], F32)
    w_out = P.dram("w_out", [D_INNER, D], F32)
    g_mlp0 = P.dram("g_mlp0", [1, D], F32)
    w1 = P.dram("w1_0", [D, DFF], F32)
    w2 = P.dram("w2_0", [DFF, D], F32)
    P.YnT = P.dram("YnT", [D_INNER, T], BF16)
    x1 = P.dram("x1", [T, D], F32)
    x2 = P.dram("x2", [T, D], F32)
    K.phase_begin()
    H = K.sb("H", [128, D_INNER], F32)
    b_H = Buf()
    P.ssm_combine(Sprev, Pprev, H, b_H)
    P.ssm_scan(True, H, b_H, None, None, dskip, normw)
    K.phase_end()
    P.dense_B(P.YnT, "YnT", 32, w_out, x_own, x1, "x1")
    P.mlp(x1, "x1", g_mlp0, w1, w2, x2, "x2")
    if with_qkv:
        g_mix1 = P.dram("g_mix1", [1, D], F32)
        w_qkv = P.dram("w_qkv", [D, QKV], F32)
        P.QT = P.dram("QT", [3072, T], BF16)
        P.KT = P.dram("KT", [3072, T], BF16)
        P.Vaug = P.dram("Vaug", [T, 48 * 128], BF16)
        P.qkv(x2, g_mix1, w_qkv)
    K.finish(P.finals)
    return P.nc


def build_L3():
    ins = ("consts", "constsf", "x2", "QT", "KT", "Vaug", "KTp", "Vp", "haloneg", "w_o", "g_mlp1", "w1_1", "w2_1",
           "g_fin")
    io = {n: "in" for n in ins}
    io.update({"y": "out"})
    P = Prog(io)
    K = P.K
    x2 = P.dram("x2", [T, D], F32)
    P.QT = P.dram("QT", [3072, T], BF16)
    P.KT = P.dram("KT", [3072, T], BF16)
    P.Vaug = P.dram("Vaug", [T, 48 * 128], BF16)
    KTp = P.dram("KTp", [3072, T], BF16)
    Vp = P.dram("Vp", [T, 48 * 128], BF16)
    haloneg = P.dram("haloneg", [128, 1], F32)
    w_o = P.dram("w_o", [AW, D], F32)
    g_mlp1 = P.dram("g_mlp1", [1, D], F32)
    w1 = P.dram("w1_1", [D, DFF], F32)
    w2 = P.dram("w2_1", [DFF, D], F32)
    g_fin = P.dram("g_fin", [1, D], F32)
    y = P.dram("y", [T, D], F32)
    P.oT = P.dram("oT", [16, 64, T], BF16)
    x3 = P.dram("x3", [T, D], F32)
    x4 = P.dram("x4", [T, D], F32)
    P.attention(KTp, Vp, haloneg)
    P.wo(w_o, x2, x3)
    P.mlp(x3, "x3", g_mlp1, w1, w2, x4, "x4")
    P.final_norm(x4, g_fin, y)
    K.finish(P.finals)
    return P.nc


def _qkv(self, x_in, g_row, w_qkv):
    K = self.K
    K.phase_begin()
    hT = K.sb("hT", [128, 16, T], BF16)
    b_hT = Buf()
    self.norm(x_in, NT, g_row, hT, b_hT)
    self.begin()
    stg = [K.sb(f"qstg{i}", [128, T], BF16) for i in range(2)]
    b_stg = [Buf() for _ in range(2)]
    ds_stg = [self.gs() for _ in range(2)]

    def epi(cbk, bi, t0, n):
        sg = cbk % 2
        K.act(stg[sg][:, t0:t0 + n], self.bank[bi][:, 0:n], AF.Copy, R=[self.b_bank[bi]], W=[b_stg[sg]])

    def post(cbk):
        sg = cbk % 2
        dst = self.QT if cbk < 24 else self.KT
        r0 = (cbk % 24) * 128
        self.store(ds_stg[sg], dst[r0:r0 + 128, :], stg[sg][:], R=[b_stg[sg]], W=[self.db("QK", cbk)])

    self.dense_A(hT, b_hT, w_qkv, 0, 6144, [(i * 512, 512) for i in range(4)], epi, post)
    self.end()
    self.begin()
    wv = w_qkv.rearrange("(kc p) n -> p kc n", p=128)
    nslot = 2
    slab = [K.sb(f"slabV{i}", [128, 16 * 512], BF16) for i in range(nslot)]
    b_slab = [Buf() for _ in range(nslot)]
    ds_slab = [self.gs() for _ in range(nslot)]
    vst = [K.sb(f"vst{i}", [128, 8, 128], BF16) for i in range(2)]
    b_vst = [Buf() for _ in range(2)]
    ds_vst = [self.gs() for _ in range(2)]
    for i in range(2):
        K.op("dve", lambda e, i=i: e.memset(vst[i][:], 1.0), W=[b_vst[i]])
    vi = 0
    for sl in range(6):
        s = sl % nslot
        sv = slab[s][:].rearrange("p (kc n) -> p kc n", kc=16)
        K.dma("pool", ds_slab[s], sv, wv[:, :, 6144 + sl * 512:6144 + (sl + 1) * 512], W=[b_slab[s]])
        for tt in range(NT):
            bi = self.rr % 4
            self.rr += 1
            for kc in range(16):
                K.mm(self.bank[bi][:], hT[:, kc, tt * 128:(tt + 1) * 128], sv[:, kc, :],
                     kc == 0, kc == 15, R=[b_hT, b_slab[s]], W=[self.b_bank[bi]])
            v2 = vi % 2
            vi += 1
            K.act(vst[v2][:, :, 0:64], self.bank[bi][:].rearrange("p (h d) -> p h d", h=8), AF.Copy,
                  R=[self.b_bank[bi]], W=[b_vst[v2]])
            self.store(ds_vst[v2], self.Vaug[tt * 128:(tt + 1) * 128, sl * 1024:(sl + 1) * 1024],
                       vst[v2][:].rearrange("p h d -> p (h d)"), R=[b_vst[v2]], W=[self.db("Vaug", sl * NT + tt)])
    self.end()
    K.phase_end()


def _attention(self, KTp, Vp, haloneg_d):
    K = self.K
    self.begin()
    NB = 2
    Qh = [K.sb(f"Qh{i}", [64, 3, T], BF16) for i in range(NB)]
    Kh = [K.sb(f"Kh{i}", [64, 3, T], BF16) for i in range(NB)]
    Kha = [K.sb(f"Kha{i}", [64, 2688], BF16) for i in range(NB)]
    Vh = [[K.sb(f"Vh{i}_{g}", [128, 16, 128], BF16) for g in range(3)] for i in range(NB)]
    Vha = [K.sb(f"Vha{i}", [128, 21, 128], BF16) for i in range(NB)]
    b_in = [Buf() for _ in range(NB)]
    ds_in = [self.gs() for _ in range(NB)]
    hneg = K.sb("hneg", [128, 1], F32)
    b_hneg = Buf()
    K.dma("sp", self.gs(), hneg[:], haloneg_d, W=[b_hneg])
    shf = K.sb("shf", [128, 64], F32)
    K.dma("sp", self.gs(), shf[:], self.cstf[:, 1152:1216], W=[b_hneg])
    tmp = [K.sb(f"atmp{i}", [128, 512], F32) for i in range(2)]
    PT = [K.sb(f"PT{i}", [128, 512], BF16) for i in range(2)]
    b_tmp = [Buf() for _ in range(2)]
    b_PT = [Buf() for _ in range(2)]
    acc2s = K.sb("acc2s", [128, T], F32)
    tot = K.sb("tot", [128, T], F32)
    recs = K.sb("recs", [128, T], F32)
    oTs = [K.sb(f"oTs{i}", [64, T], BF16) for i in range(2)]
    b_acc2s, b_tot, b_recs = Buf(), Buf(), Buf()
    b_oTs = [Buf() for _ in range(2)]
    ds_oTs = [self.gs() for _ in range(2)]
    K.op("dve", lambda e: e.memset(recs[:], 0.0), W=[b_recs])
    DILS = (1, 4, 16)

    def load(h):
        s = h % NB
        items = []
        for g in range(3):
            r0 = (g * 16 + h) * 64
            items.append((Qh[s][:, g, :], self.QT[r0:r0 + 64, :]))
            items.append((Kh[s][:, g, :], self.KT[r0:r0 + 64, :]))
            cs = slice((g * 16 + h) * 128, (g * 16 + h + 1) * 128)
            vsrc = self.Vaug[:, cs]
            if g == 0:
                items.append((Vh[s][0][:], vsrc.rearrange("(b j) c -> j b c", j=128)))
                items.append((Kha[s][:, 0:128], KTp[r0:r0 + 64, 1920:2048]))
                items.append((Vha[s][:, 0, :], Vp[1920:2048, cs]))
            elif g == 1:
                for n in range(4):
                    items.append((Vh[s][1][:, n * 4:(n + 1) * 4, :],
                                  vsrc[n * 512:(n + 1) * 512, :].rearrange("(j r) c -> j r c", r=4)))
                items.append((Kha[s][:, 128:640], KTp[r0:r0 + 64, 1536:2048]))
                items.append((Vha[s][:, 1:5, :], Vp[1536:2048, cs].rearrange("(j r) c -> j r c", r=4)))
            else:
                items.append((Vh[s][2][:], vsrc.rearrange("(j r) c -> j r c", r=16)))
                items.append((Kha[s][:, 640:2688], KTp[r0:r0 + 64, :]))
                items.append((Vha[s][:, 5:21, :], Vp[:, cs].rearrange("(j r) c -> j r c", r=16)))
        K.dma_group("sp", ds_in[s], items, W=[b_in[s]])

    def blk(ap2d, g, b):
        if g == 0:
            return ap2d[:, b * 128:(b + 1) * 128]
        if g == 1:
            return ap2d.rearrange("p (n j r) -> p n r j", j=128, r=4)[:, b // 4, b % 4, :]
        return ap2d.rearrange("p (j r) -> p r j", r=16)[:, b, :]

    def halo_k(s, g, b):
        if g == 0:
            return Kha[s][:, 0:128]
        if g == 1:
            return Kha[s][:, 128:640].rearrange("p (j r) -> p r j", r=4)[:, b, :]
        return Kha[s][:, 640:2688].rearrange("p (j r) -> p r j", r=16)[:, b, :]

    def halo_v(s, g, b):
        return Vha[s][:, (0, 1, 5)[g] + b, :]

    def acc_out(g, b, epoch2):
        if g == 2:
            return 4 + b // 4, self.bank[4 + b // 4][:, (b % 4) * 128:(b % 4 + 1) * 128]
        if g == 0:
            return 4 + b // 4, self.bank[4 + b // 4][:, (b % 4) * 128:(b % 4 + 1) * 128]
        n, r = b // 4, b % 4
        return 4 + n, self.bank[4 + n][:].rearrange("p (j r) -> p r j", r=4)[:, r, :]

    bt = {"i": 0}
    load(0)
    for h in range(16):
        s = h % NB
        if h + 1 < 16:
            load(h + 1)
        R_in = [b_in[s]]
        for epoch in (0, 1):
            started = set()
            groups = (2,) if epoch == 0 else (0, 1)
            for g in groups:
                slope = 2.0 ** (-8.0 * (g * 16 + h + 1) / 48.0)
                cmul = -8.0 * slope * DILS[g]
                batches = []
                for b0 in range(0, 16, 4):
                    batches.append(("cur", [(b, ("own", b)) for b in range(b0, b0 + 4)]))
                if g == 0:
                    prs = [(b, ("own", b - 1)) for b in range(1, 16)]
                    hal = [(0, ("halo", 0))]
                elif g == 1:
                    prs = [(b, ("own", b - 4)) for b in range(4, 16)]
                    hal = [(b, ("halo", b)) for b in range(4)]
                else:
                    prs = []
                    hal = [(b, ("halo", b)) for b in range(16)]
                for i in range(0, len(prs), 4):
                    batches.append(("prev", prs[i:i + 4]))
                for i in range(0, len(hal), 4):
                    batches.append(("halo", hal[i:i + 4]))
                for kind, prl in batches:
                    i2 = bt["i"] % 2
                    bt["i"] += 1
                    sb_ = i2
                    n = len(prl)
                    for i, (qb, (src, kb)) in enumerate(prl):
                        kap = blk(Kh[s][:, g, :], g, kb) if src == "own" else halo_k(s, g, kb)
                        K.mm(self.bank[sb_][:, i * 128:(i + 1) * 128], kap, blk(Qh[s][:, g, :], g, qb),
                             R=R_in, W=[self.b_bank[sb_]])
                    dcol = 512 if kind == "cur" else 0
                    K.stt(tmp[i2][:, 0:n * 128], self.dist[:, dcol:dcol + n * 128], cmul,
                          self.bank[sb_][:, 0:n * 128], ALU.mult, ALU.add,
                          R=[self.b_bank[sb_], self.b_cst], W=[b_tmp[i2]])
                    if kind == "halo":
                        K.act(PT[i2][:, 0:n * 128], tmp[i2][:, 0:n * 128], AF.Exp, R=[b_tmp[i2], b_hneg],
                              W=[b_PT[i2]], scale=0.125, bias=hneg[:])
                    else:
                        K.act(PT[i2][:, 0:n * 128], tmp[i2][:, 0:n * 128], AF.Exp, R=[b_tmp[i2]],
                              W=[b_PT[i2]], scale=0.125)
                    for i, (qb, (src, kb)) in enumerate(prl):
                        vap = Vh[s][g][:, kb, :] if src == "own" else halo_v(s, g, kb)
                        bidx, oap = acc_out(g, qb, epoch)
                        K.mm(oap, vap, PT[i2][:, i * 128:(i + 1) * 128], start=(bidx not in started), stop=True,
                             R=R_in + [b_PT[i2]], W=[self.b_bank[bidx]])
                        started.add(bidx)
            if epoch == 0:
                for bq in range(4):
                    K.act(acc2s[:].rearrange("p (j r) -> p r j", r=16)[:, 4 * bq:4 * bq + 4, :],
                          self.bank[4 + bq][:].rearrange("p (r j) -> p r j", r=4), AF.Copy,
                          R=[self.b_bank[4 + bq]], W=[b_acc2s])
            else:
                for n in range(4):
                    K.tt("dve", tot[:, n * 512:(n + 1) * 512], self.bank[4 + n][:], acc2s[:, n * 512:(n + 1) * 512],
                         ALU.add, R=[self.b_bank[4 + n], b_acc2s], W=[b_tot])
        K.op("dve", lambda e: e.reciprocal(out=recs[64:128, :], in_=tot[64:128, :]), R=[b_tot], W=[b_recs])
        o2 = h % 2
        for n in range(4):
            sb_ = 2 + n % 2
            K.mm(self.bank[sb_][0:64, :], shf[:], recs[:, n * 512:(n + 1) * 512], R=[b_recs, b_hneg],
                 W=[self.b_bank[sb_]])
            K.tt("dve", oTs[o2][:, n * 512:(n + 1) * 512], tot[0:64, n * 512:(n + 1) * 512], self.bank[sb_][0:64, :],
                 ALU.mult, R=[b_tot, self.b_bank[sb_]], W=[b_oTs[o2]])
        self.store(ds_oTs[o2], self.oT[h], oTs[o2][:], R=[b_oTs[o2]], W=[self.db("oT", h)])
    self.end()


def _wo(self, w_o, res_d, out_d):
    K = self.K
    self.begin()
    oTa = K.sb("oTa", [64, 16, T], BF16)
    b_oTa = Buf()
    K.dma_group("sp", self.gs(), [(oTa[:, h, :], self.oT[h]) for h in range(16)],
                R=[self.db("oT", h) for h in range(16)], W=[b_oTa])
    slab = [K.sb(f"slabO{i}", [64, 16, 512], BF16) for i in range(2)]
    b_slab = [Buf() for _ in range(2)]
    ds_slab = [self.gs() for _ in range(2)]
    xr = [K.sb(f"xr{i}", [128, 512], F32) for i in range(2)]
    b_xr = [Buf() for _ in range(2)]
    ds_xr = [self.gs() for _ in range(2)]
    ost = [K.sb(f"ost{i}", [128, 512], F32) for i in range(2)]
    b_ost = [Buf() for _ in range(2)]
    ds_ost = [self.gs() for _ in range(2)]
    wv = w_o.rearrange("(h d) n -> d h n", d=64)
    ep = 0
    for cb in range(4):
        s = cb % 2
        K.dma("pool", ds_slab[s], slab[s][:], wv[:, :, cb * 512:(cb + 1) * 512], W=[b_slab[s]])
        for tt in range(NT):
            bi = self.rr % 4
            self.rr += 1
            for h in range(16):
                K.mm(self.bank[bi][:], oTa[:, h, tt * 128:(tt + 1) * 128], slab[s][:, h, :], h == 0, h == 15,
                     R=[b_oTa, b_slab[s]], W=[self.b_bank[bi]])
            e2 = ep % 2
            ep += 1
            K.dma("sp", ds_xr[e2], xr[e2][:], res_d[tt * 128:(tt + 1) * 128, cb * 512:(cb + 1) * 512], W=[b_xr[e2]])
            K.tt("dve", ost[e2][:], self.bank[bi][:], xr[e2][:], ALU.add, R=[self.b_bank[bi], b_xr[e2]], W=[b_ost[e2]])
            self.store(ds_ost[e2], out_d[tt * 128:(tt + 1) * 128, cb * 512:(cb + 1) * 512], ost[e2][:],
                       R=[b_ost[e2]], W=[self.db("x3", cb * NT + tt)])
    self.end()


def _final_norm(self, x_d, g_row, y_d):
    K = self.K
    self.begin()
    gB, b_gB = self.bcast_load("gBf", g_row, D)
    xt = [K.sb(f"fxt{i}", [128, D], F32) for i in range(2)]
    yo = [K.sb(f"fyo{i}", [128, D], F32) for i in range(2)]
    ssq = [K.sb(f"fssq{i}", [128, 1], F32) for i in range(2)]
    rs = [K.sb(f"frs{i}", [128, 1], F32) for i in range(2)]
    b_xt = [Buf() for _ in range(2)]
    b_yo = [Buf() for _ in range(2)]
    b_sm = [Buf() for _ in range(2)]
    dsx = [self.gs() for _ in range(2)]
    dsy = [self.gs() for _ in range(2)]
    for t in range(NT):
        s = t % 2
        K.dma("sp", dsx[s], xt[s][:], x_d[t * 128:(t + 1) * 128, :], W=[b_xt[s]])
        K.act(yo[s][:], xt[s][:], AF.Square, R=[b_xt[s]], W=[b_yo[s], b_sm[s]], accum_out=ssq[s][:])
        K.act(rs[s][:], ssq[s][:], AF.Sqrt, R=[b_sm[s], self.b_cst], W=[b_sm[s]], scale=1.0 / D, bias=self.eps_t[:])
        K.op("dve", lambda e, s=s: e.reciprocal(out=rs[s][:], in_=rs[s][:]), R=[b_sm[s]], W=[b_sm[s]])
        K.stt(yo[s][:], xt[s][:], rs[s][:], gB[:], ALU.mult, ALU.mult, R=[b_xt[s], b_sm[s], b_gB], W=[b_yo[s]])
        self.store(dsy[s], y_d[t * 128:(t + 1) * 128, :], yo[s][:], R=[b_yo[s]], W=[self.db("y", t)])
    self.end()


Prog.qkv = _qkv
Prog.attention = _attention
Prog.wo = _wo
Prog.final_norm = _final_norm


_PROGS = {}


def _prog(name):
    if name not in _PROGS:
        _PROGS[name] = {"L1": build_L1, "L2": build_L2, "L3": build_L3}[name]()
    return _PROGS[name]


def _row(a):
    return np.ascontiguousarray(np.asarray(a, dtype=np.float32).reshape(1, -1))


def kernel(**inputs):
    f = lambda k: np.asarray(inputs[k], dtype=np.float32)
    x = f("x")
    cst, cstf = _consts()
    cores = list(range(NCORES))
    cw = f("ssm_conv_w")[0]
    cb = f("ssm_conv_b")[0]
    conv_wT = np.ascontiguousarray(cw.reshape(4, 48, 128).transpose(2, 1, 0).reshape(128, 192))
    conv_bT = np.ascontiguousarray(cb.reshape(48, 128).T)
    norm_mix, norm_mlp = f("norm_mix"), f("norm_mlp")
    w_in = np.ascontiguousarray(f("ssm_w_in")[0])
    maps = []
    for c in cores:
        b, q = c // 4, c % 4
        xe = np.zeros((TE, D), np.float32)
        if q > 0:
            xe[:128] = x[b, q * T - 128:q * T]
        xe[128:] = x[b, q * T:(q + 1) * T]
        maps.append({"consts": cst, "constsf": cstf, "x_ext": xe, "g0": _row(norm_mix[0]), "w_in": w_in,
                     "conv_wT": conv_wT, "conv_bT": conv_bT, "dtb": _row(f("ssm_dt_bias")[0]),
                     "alog": _row(f("ssm_a_log")[0])})
    r1 = run_bass_kernel_spmd(_prog("L1"), maps, core_ids=cores).results
    w_out = np.ascontiguousarray(f("ssm_w_out")[0])
    w1_0 = np.ascontiguousarray(f("mlp_w1")[0])
    w2_0 = np.ascontiguousarray(f("mlp_w2")[0])
    w_qkv = np.ascontiguousarray(f("attn_w_qkv")[0])
    maps = []
    for c in cores:
        b, q = c // 4, c % 4
        Sprev = np.zeros((3, 128, D_INNER), np.float32)
        Pprev = np.zeros((3, 128, NH), np.float32)
        for i, p in enumerate(range(q - 3, q)):
            if p >= 0:
                Sprev[i] = np.asarray(r1[b * 4 + p]["Sloc"])
                Pprev[i] = np.asarray(r1[b * 4 + p]["Ptot"])
        m = {"consts": cst, "constsf": cstf, "Sprev": Sprev, "Pprev": Pprev,
             "x_own": np.ascontiguousarray(x[b, q * T:(q + 1) * T]), "dskip": _row(f("ssm_d")[0]),
             "normw": _row(f("ssm_norm_w")[0]), "w_out": w_out, "g_mlp0": _row(norm_mlp[0]), "w1_0": w1_0,
             "w2_0": w2_0, "g_mix1": _row(norm_mix[1]), "w_qkv": w_qkv}
        for k in ("Zs", "Xs", "Bt", "BT", "CT", "dt_s", "a_s"):
            m[k] = np.asarray(r1[c][k])
        maps.append(m)
    r2 = run_bass_kernel_spmd(_prog("L2"), maps, core_ids=cores).results
    del r1
    w_o = np.ascontiguousarray(f("attn_w_o")[0])
    w1_1 = np.ascontiguousarray(f("mlp_w1")[1])
    w2_1 = np.ascontiguousarray(f("mlp_w2")[1])
    maps = []
    for c in cores:
        b, q = c // 4, c % 4
        m = {"consts": cst, "constsf": cstf, "w_o": w_o, "g_mlp1": _row(norm_mlp[1]), "w1_1": w1_1, "w2_1": w2_1,
             "g_fin": _row(f("final_norm"))}
        for k in ("x2", "QT", "KT", "Vaug"):
            m[k] = np.asarray(r2[c][k])
        if q > 0:
            m["KTp"] = np.asarray(r2[c - 1]["KT"])
            m["Vp"] = np.asarray(r2[c - 1]["Vaug"])
            m["haloneg"] = np.zeros((128, 1), np.float32)
        else:
            m["KTp"] = np.zeros_like(m["KT"])
            m["Vp"] = np.zeros_like(m["Vaug"])
            m["haloneg"] = np.full((128, 1), -30000.0, np.float32)
        maps.append(m)
    r3 = run_bass_kernel_spmd(_prog("L3"), maps, core_ids=cores).results
    out = np.empty((2, 4 * T, D), np.float32)
    for c in cores:
        out[c // 4, (c % 4) * T:(c % 4 + 1) * T] = np.asarray(r3[c]["y"])
    return out
```
